# Optimizing a Trainium2 kernel written in Bass

```python
import math
import jax, jax.numpy as jnp
from jax import lax
import numpy as np

D_MODEL = 4096
BATCH = 4
SEQ = 4096
DEPTH = 2

HEAD_DIM = 128
BLOCK_Q = 128
SSM_GROUP = 16
SSM_STATE = 64
W_SSM = D_MODEL // 2
SSM_GROUPS = W_SSM // SSM_GROUP
SB_HEADS = (D_MODEL - W_SSM) // HEAD_DIM
W_SB = SB_HEADS * HEAD_DIM
EVEN_IN = W_SSM + 3 * W_SB
EVEN_OUT = W_SSM + W_SB
FOX_HEADS = D_MODEL // HEAD_DIM
FOX_WIDTH = FOX_HEADS * HEAD_DIM
FOX_IN = 3 * FOX_WIDTH + FOX_HEADS
D_FF = 4 * D_MODEL
N_EVEN = (DEPTH + 1) // 2
N_ODD = DEPTH // 2
DEEPNORM_ALPHA = (2.0 * DEPTH) ** 0.25
DEEPNORM_BETA = (8.0 * DEPTH) ** -0.25
LN_EPS = 1e-5
DT_MIN = 1e-3
DT_MAX = 1e-1
FORGET_BIAS_INIT = 2.0

kernel_name = "hybrid_s5_stickbreak_fox_deepnorm"


def layer_norm(x, g, b):
    xf = x.astype(jnp.float32)
    mu = jnp.mean(xf, axis=-1, keepdims=True)
    var = jnp.mean(jnp.square(xf - mu), axis=-1, keepdims=True)
    y = (xf - mu) * lax.rsqrt(var + LN_EPS) * g.astype(jnp.float32) + b.astype(jnp.float32)
    return y.astype(x.dtype)


def _split_heads(t, n_heads):
    b, s, _ = t.shape
    return t.reshape(b, s, n_heads, HEAD_DIM).transpose(0, 2, 1, 3)


def _to_blocks(t):
    b, h, s = t.shape[:3]
    nb = s // BLOCK_Q
    t = t.reshape((b, h, nb, BLOCK_Q) + t.shape[3:])
    return jnp.moveaxis(t, 2, 0)


def _merge_blocks(o):
    nb, b, h, blk, dh = o.shape
    return o.transpose(1, 0, 3, 2, 4).reshape(b, nb * blk, h * dh)


def _complex_affine_combine(e1, e2):
    a1r, a1i, b1r, b1i = e1
    a2r, a2i, b2r, b2i = e2
    ar = a2r * a1r - a2i * a1i
    ai = a2r * a1i + a2i * a1r
    br = a2r * b1r - a2i * b1i + b2r
    bi = a2r * b1i + a2i * b1r + b2i
    return ar, ai, br, bi


def s5_mixer(u, a_re, a_im, log_dt, b_re, b_im, c_re, c_im, d, w_glu):
    bsz, s, _ = u.shape
    uf = u.astype(jnp.float32).reshape(bsz, s, SSM_GROUPS, SSM_GROUP)
    lr = a_re.astype(jnp.float32)
    li = a_im.astype(jnp.float32)
    dt = jnp.exp(log_dt.astype(jnp.float32))[:, None]
    mag = jnp.exp(lr * dt)
    abar_r = mag * jnp.cos(li * dt)
    abar_i = mag * jnp.sin(li * dt)
    den = lr * lr + li * li
    nr = abar_r - 1.0
    ni = abar_i
    zr = (nr * lr + ni * li) / den
    zi = (ni * lr - nr * li) / den
    br = b_re.astype(jnp.float32)
    bi = b_im.astype(jnp.float32)
    bbar_r = zr[..., None] * br - zi[..., None] * bi
    bbar_i = zr[..., None] * bi + zi[..., None] * br
    bu_r = jnp.einsum('bsgp,gnp->bsgn', uf, bbar_r)
    bu_i = jnp.einsum('bsgp,gnp->bsgn', uf, bbar_i)
    ar = jnp.broadcast_to(abar_r, bu_r.shape)
    ai = jnp.broadcast_to(abar_i, bu_i.shape)
    _, _, xr, xi = lax.associative_scan(_complex_affine_combine, (ar, ai, bu_r, bu_i), axis=1)
    y = (jnp.einsum('bsgn,gpn->bsgp', xr, c_re.astype(jnp.float32))
         - jnp.einsum('bsgn,gpn->bsgp', xi, c_im.astype(jnp.float32))
         + d.astype(jnp.float32).reshape(SSM_GROUPS, SSM_GROUP) * uf)
    y = y.reshape(bsz, s, W_SSM).astype(u.dtype)
    g = jax.nn.gelu(y)
    return g * jax.nn.sigmoid(g @ w_glu)


def stick_breaking_attention(q, k, v):
    s = q.shape[2]
    nb = s // BLOCK_Q
    scale = HEAD_DIM ** -0.5
    key_pos = jnp.arange(s)

    def one_block(args):
        q_blk, blk = args
        z = jnp.einsum('bhqd,bhkd->bhqk', q_blk, k).astype(jnp.float32) * scale
        q_pos = blk * BLOCK_Q + jnp.arange(BLOCK_Q)
        earlier = key_pos[None, :] < q_pos[:, None]
        log_keep = jnp.where(earlier, jax.nn.log_sigmoid(-z), 0.0)
        log_between = lax.cumsum(log_keep, axis=3, reverse=True) - log_keep
        w = jnp.where(earlier, jnp.exp(jax.nn.log_sigmoid(z) + log_between), 0.0)
        return jnp.einsum('bhqk,bhkd->bhqd', w.astype(v.dtype), v)

    out = lax.map(one_block, (_to_blocks(q), jnp.arange(nb)))
    return _merge_blocks(out)


def forgetting_attention(q, k, v, cum):
    s = q.shape[2]
    nb = s // BLOCK_Q
    scale = HEAD_DIM ** -0.5
    key_pos = jnp.arange(s)

    def one_block(args):
        q_blk, c_blk, blk = args
        logits = (jnp.einsum('bhqd,bhkd->bhqk', q_blk, k).astype(jnp.float32) * scale
                  + c_blk[..., :, None] - cum[:, :, None, :])
        q_pos = blk * BLOCK_Q + jnp.arange(BLOCK_Q)
        causal = key_pos[None, :] <= q_pos[:, None]
        p = jax.nn.softmax(jnp.where(causal, logits, -jnp.inf), axis=-1)
        return jnp.einsum('bhqk,bhkd->bhqd', p.astype(v.dtype), v)

    out = lax.map(one_block, (_to_blocks(q), _to_blocks(cum), jnp.arange(nb)))
    return _merge_blocks(out)


def even_mixer(x, w_in, a_re, a_im, log_dt, b_re, b_im, c_re, c_im, d, w_glu, w_out):
    proj = x @ w_in
    u = proj[..., :W_SSM]
    q, k, v = jnp.split(proj[..., W_SSM:], 3, axis=-1)
    y_ssm = s5_mixer(u, a_re, a_im, log_dt, b_re, b_im, c_re, c_im, d, w_glu)
    y_sb = stick_breaking_attention(_split_heads(q, SB_HEADS), _split_heads(k, SB_HEADS),
                                    _split_heads(v, SB_HEADS))
    return jnp.concatenate([y_ssm, y_sb.astype(y_ssm.dtype)], axis=-1) @ w_out


def odd_mixer(x, w_in, b_f, w_out):
    proj = x @ w_in
    q, k, v = jnp.split(proj[..., :3 * FOX_WIDTH], 3, axis=-1)
    f_logit = proj[..., 3 * FOX_WIDTH:].astype(jnp.float32) + b_f.astype(jnp.float32)
    cum = jnp.cumsum(jax.nn.log_sigmoid(f_logit), axis=1).transpose(0, 2, 1)
    y = forgetting_attention(_split_heads(q, FOX_HEADS), _split_heads(k, FOX_HEADS),
                             _split_heads(v, FOX_HEADS), cum)
    return y.astype(x.dtype) @ w_out


def setup_inputs(seed: int = 0) -> dict:
    key = jax.random.key(seed)
    ks = jax.random.split(key, 21)
    nrm = jax.random.normal
    n_idx = jnp.arange(SSM_STATE, dtype=jnp.float32)
    return {
        'x': nrm(ks[0], (BATCH, SEQ, D_MODEL), jnp.float32),
        'even_w_in': nrm(ks[1], (N_EVEN, D_MODEL, EVEN_IN), jnp.float32) * D_MODEL ** -0.5,
        'ssm_a_re': -0.5 + 0.01 * nrm(ks[2], (N_EVEN, SSM_GROUPS, SSM_STATE), jnp.float32),
        'ssm_a_im': math.pi * n_idx + 0.01 * nrm(ks[3], (N_EVEN, SSM_GROUPS, SSM_STATE), jnp.float32),
        'ssm_log_dt': jax.random.uniform(ks[4], (N_EVEN, SSM_GROUPS), jnp.float32,
                                         minval=math.log(DT_MIN), maxval=math.log(DT_MAX)),
        'ssm_b_re': nrm(ks[5], (N_EVEN, SSM_GROUPS, SSM_STATE, SSM_GROUP), jnp.float32) * (2 * SSM_GROUP) ** -0.5,
        'ssm_b_im': nrm(ks[6], (N_EVEN, SSM_GROUPS, SSM_STATE, SSM_GROUP), jnp.float32) * (2 * SSM_GROUP) ** -0.5,
        'ssm_c_re': nrm(ks[7], (N_EVEN, SSM_GROUPS, SSM_GROUP, SSM_STATE), jnp.float32) * (2 * SSM_STATE) ** -0.5,
        'ssm_c_im': nrm(ks[8], (N_EVEN, SSM_GROUPS, SSM_GROUP, SSM_STATE), jnp.float32) * (2 * SSM_STATE) ** -0.5,
        'ssm_d': nrm(ks[9], (N_EVEN, W_SSM), jnp.float32),
        'ssm_w_glu': nrm(ks[10], (N_EVEN, W_SSM, W_SSM), jnp.float32) * W_SSM ** -0.5,
        'even_w_out': nrm(ks[11], (N_EVEN, EVEN_OUT, D_MODEL), jnp.float32) * EVEN_OUT ** -0.5 * DEEPNORM_BETA,
        'fox_w_in': nrm(ks[12], (N_ODD, D_MODEL, FOX_IN), jnp.float32) * D_MODEL ** -0.5,
        'fox_b_f': FORGET_BIAS_INIT + 0.1 * nrm(ks[13], (N_ODD, FOX_HEADS), jnp.float32),
        'fox_w_out': nrm(ks[14], (N_ODD, FOX_WIDTH, D_MODEL), jnp.float32) * FOX_WIDTH ** -0.5 * DEEPNORM_BETA,
        'ln_mix_g': 1.0 + 0.01 * nrm(ks[15], (DEPTH, D_MODEL), jnp.float32),
        'ln_mix_b': 0.01 * nrm(ks[16], (DEPTH, D_MODEL), jnp.float32),
        'mlp_w1': nrm(ks[17], (DEPTH, D_MODEL, D_FF), jnp.float32) * D_MODEL ** -0.5,
        'mlp_w2': nrm(ks[18], (DEPTH, D_FF, D_MODEL), jnp.float32) * D_FF ** -0.5 * DEEPNORM_BETA,
        'ln_ffn_g': 1.0 + 0.01 * nrm(ks[19], (DEPTH, D_MODEL), jnp.float32),
        'ln_ffn_b': 0.01 * nrm(ks[20], (DEPTH, D_MODEL), jnp.float32),
    }


def reference(x, even_w_in, ssm_a_re, ssm_a_im, ssm_log_dt, ssm_b_re, ssm_b_im, ssm_c_re,
              ssm_c_im, ssm_d, ssm_w_glu, even_w_out, fox_w_in, fox_b_f, fox_w_out,
              ln_mix_g, ln_mix_b, mlp_w1, mlp_w2, ln_ffn_g, ln_ffn_b):
    h = x
    for layer in range(DEPTH):
        i = layer // 2
        if layer % 2 == 0:
            mix = even_mixer(h, even_w_in[i], ssm_a_re[i], ssm_a_im[i], ssm_log_dt[i],
                             ssm_b_re[i], ssm_b_im[i], ssm_c_re[i], ssm_c_im[i], ssm_d[i],
                             ssm_w_glu[i], even_w_out[i])
        else:
            mix = odd_mixer(h, fox_w_in[i], fox_b_f[i], fox_w_out[i])
        h = layer_norm(DEEPNORM_ALPHA * h + mix.astype(h.dtype), ln_mix_g[layer], ln_mix_b[layer])
        ff = jnp.square(jax.nn.relu(h @ mlp_w1[layer])) @ mlp_w2[layer]
        h = layer_norm(DEEPNORM_ALPHA * h + ff, ln_ffn_g[layer], ln_ffn_b[layer])
    return h
```

```python
import contextlib
import numpy as np
import concourse.bass as bass
import concourse.mybir as mybir
from concourse.bass_utils import run_bass_kernel_spmd

F32 = mybir.dt.float32
BF16 = mybir.dt.bfloat16
ALU = mybir.AluOpType
AF = mybir.ActivationFunctionType

D = 4096
S = 4096
B = 4
NCORES = 8
TT = 512
ENGS = ("pe", "act", "dve", "pool", "sp")


class Sched:
    def __init__(self, nc, stack):
        self.nc = nc
        self.stack = stack
        self.lists = {e: [] for e in ENGS}
        self.count = {e: 0 for e in ENGS}
        self.sem = {e: stack.enter_context(nc.semaphore("s_" + e)) for e in ENGS}
        self.seen = {e: {} for e in ENGS}
        self.dsem_cnt = {}
        self.nsem = 0

    def new_sem(self, name=None):
        self.nsem += 1
        s = self.stack.enter_context(self.nc.semaphore(name or ("d%d" % self.nsem)))
        self.dsem_cnt[id(s)] = 0
        return s

    def _waits(self, eng, waits):
        out = []
        seen = self.seen[eng]
        for t in waits:
            if t is None:
                continue
            sem, val = t
            k = id(sem)
            if seen.get(k, 0) >= val:
                continue
            seen[k] = val
            out.append((sem, val))
        return out

    def op(self, eng, fn, waits=()):
        w = self._waits(eng, waits)
        self.count[eng] += 1
        tok = (self.sem[eng], self.count[eng])
        self.lists[eng].append((w, fn, (self.sem[eng], 1)))
        return tok

    def dma(self, eng, fn, dsem, waits=()):
        w = self._waits(eng, waits)
        self.dsem_cnt[id(dsem)] += 16
        tok = (dsem, self.dsem_cnt[id(dsem)])
        self.lists[eng].append((w, fn, (dsem, 16)))
        return tok

    def cc(self, fn, ccsem, waits=()):
        w = self._waits("pool", waits)
        self.dsem_cnt[id(ccsem)] += 1
        tok = (ccsem, self.dsem_cnt[id(ccsem)])
        self.lists["pool"].append((w, fn, (ccsem, 1)))
        return tok

    def wait_only(self, eng, waits):
        w = self._waits(eng, waits)
        if w:
            self.lists[eng].append((w, None, None))

    def emit(self):
        nc = self.nc
        handles = {"pe": "tensor", "act": "scalar", "dve": "vector", "pool": "gpsimd", "sp": "sync"}
        with nc.Block() as block:
            for e in ENGS:
                lst = self.lists[e]

                def body(eng, lst=lst):
                    for w, fn, inc in lst:
                        for sem, val in w:
                            eng.wait_ge(sem, val)
                        if fn is not None:
                            ins = fn(eng)
                            ins.then_inc(inc[0], inc[1])

                getattr(block, handles[e])(body)
        self.lists = {e: [] for e in ENGS}


class Slots:
    def __init__(self, tiles):
        self.tiles = tiles
        self.free_tok = [[] for _ in tiles]
        self.i = 0

    def next(self):
        k = self.i % len(self.tiles)
        self.i += 1
        return k


def sb(nc, stack, name, shape, dt):
    return stack.enter_context(nc.sbuf_tensor(name, list(shape), dt))


def ps(nc, stack, name, shape, dt=F32):
    return stack.enter_context(nc.psum_tensor(name, list(shape), dt))


def gemm_group(S_, hT, kch, w_dram, col0, wslots, wsem, pbanks, mode, first_waits):
    nc = S_.nc
    KB = wslots.tiles[0].shape[1]
    last = [None] * 4
    wview = w_dram.rearrange("(kc p) n -> p kc n", p=128)
    for kb in range(kch // KB):
        k = wslots.next()
        wt = wslots.tiles[k]
        tokw = S_.dma(
            "pool",
            lambda e, wt=wt, kb=kb: e.dma_start(out=wt[:, :, :], in_=wview[:, kb * KB:(kb + 1) * KB, col0:col0 + 512]),
            wsem[k], waits=wslots.free_tok[k])
        rd = []
        for kk in range(KB):
            kc = kb * KB + kk
            for j in range(4):
                w = [tokw] if (kk == 0 and j == 0) else []
                if kc == 0:
                    w = w + list(first_waits[j])
                if mode == "fm":
                    fn = (lambda e, j=j, kk=kk, kc=kc, wt=wt: e.matmul(
                        pbanks[j][:, :], lhsT=wt[:, kk, j * 128:(j + 1) * 128], rhs=hT[:, kc, :],
                        start=(kc == 0), stop=(kc == kch - 1)))
                else:
                    fn = (lambda e, j=j, kk=kk, kc=kc, wt=wt: e.matmul(
                        pbanks[j][:, :], lhsT=hT[:, kc, j * 128:(j + 1) * 128], rhs=wt[:, kk, :],
                        start=(kc == 0), stop=(kc == kch - 1)))
                t = S_.op("pe", fn, waits=w)
                last[j] = t
        wslots.free_tok[k] = [t]
    return last


def emit_a0(S_, nc, st, xT, w, T, io):
    NO = 8192
    hT = sb(nc, st, "hT", [128, 32, TT], BF16)
    wslots = Slots([sb(nc, st, "a0w%d" % i, [128, 8, 512], BF16) for i in range(4)])
    wsem = [S_.new_sem() for _ in range(4)]
    pbanks = [ps(nc, st, "a0pb%d" % i, [128, 512]) for i in range(4)]
    o32 = Slots([sb(nc, st, "a0o32_%d" % i, [128, 4, 512], F32) for i in range(2)])
    o16 = Slots([sb(nc, st, "a0o16_%d" % i, [128, 4, 512], BF16) for i in range(2)])
    osem32 = [S_.new_sem() for _ in range(2)]
    osem16 = [S_.new_sem() for _ in range(2)]
    hsem = S_.new_sem()
    bank_free = [[] for _ in range(4)]
    h_free = []
    xv = xT.rearrange("(kc p) t -> p kc t", p=128)
    for tt in range(T // TT):
        t0 = tt * TT
        tile_toks = []
        tokh = S_.dma("pool", lambda e, t0=t0: e.dma_start(out=hT[:, :, :], in_=xv[:, :, t0:t0 + TT]),
                      hsem, waits=h_free)
        for g in range(NO // 512):
            col0 = g * 512
            mode = "tm" if col0 >= 6144 else "fm"
            fw = [[tokh] + bank_free[j] for j in range(4)]
            last = gemm_group(S_, hT, 32, w, col0, wslots, wsem, pbanks, mode, fw)
            if g == NO // 512 - 1:
                h_free = list(last)
            if col0 < 2048:
                slots, osem = o32, osem32
            else:
                slots, osem = o16, osem16
            k = slots.next()
            ot = slots.tiles[k]
            evs = []
            for j in range(4):
                eng = "act" if j % 2 == 0 else "dve"
                if eng == "act":
                    fn = lambda e, j=j, ot=ot: e.activation(out=ot[:, j, :], in_=pbanks[j][:, :], func=AF.Copy)
                else:
                    fn = lambda e, j=j, ot=ot: e.tensor_copy(out=ot[:, j, :], in_=pbanks[j][:, :])
                t = S_.op(eng, fn, waits=[last[j]] + slots.free_tok[k])
                bank_free[j] = [t]
                evs.append(t)
            dv = io["a0_dst"](col0, tt)
            td = S_.dma("sp", lambda e, dv=dv, ot=ot: e.dma_start(out=dv, in_=ot[:, :, :]), osem[k], waits=evs)
            slots.free_tok[k] = [td]
            tile_toks.append(td)
        io["a0_after_tile"](tt, tile_toks)
    S_.wait_only("sp", [(sm, S_.dsem_cnt[id(sm)]) for sm in osem32 + osem16])


ALPHA = (2.0 * 2) ** 0.25
LN_EPS = 1e-5
FOX_IN = 3 * 4096 + 32


def emit_c(S_, nc, st, layer, T, DR, io):
    hres, w_out, lnp, w1, w2, hout, zbuf, hmid = (DR[k] for k in ("hres", "w_out", "lnp", "w1", "w2", "hout", "zbuf", "hmid"))
    if layer == 0:
        w_glu, w_fox = DR["w_glu"], DR["w_fox"]
    if True:
        actA = sb(nc, st, "actA%d" % layer, [128, 32, TT], BF16)
        hid = sb(nc, st, "hid%d" % layer, [128, 128, TT], BF16)
        wslots = Slots([sb(nc, st, "wsl%d_%d" % (layer, i), [128, 4, 512], BF16) for i in range(4)])
        wsem = [S_.new_sem() for _ in range(4)]
        pbanks = [ps(nc, st, "cpb%d_%d" % (layer, i), [128, 512]) for i in range(4)]
        st_sum = ps(nc, st, "st_sum%d" % layer, [128, 512])
        st_sq = ps(nc, st, "st_sq%d" % layer, [128, 512])
        NS = 2
        f32s = {nm: Slots([sb(nc, st, "%s%d_%d" % (nm, layer, i), [128, 512], F32) for i in range(n)])
                for nm, n in (("res", NS), ("zt", NS), ("zsq", 2), ("tmp", 2), ("tmp2", 2))}
        f32sem = {nm: [S_.new_sem() for _ in range(len(f32s[nm].tiles))] for nm in ("res", "zt", "tmp")}
        o16 = Slots([sb(nc, st, "co16_%d_%d" % (layer, i), [128, 512], BF16) for i in range(2)])
        o16sem = [S_.new_sem() for _ in range(2)]
        mean = sb(nc, st, "mean%d" % layer, [128, 512], F32)
        rstd = sb(nc, st, "rstd%d" % layer, [128, 512], F32)
        ones = sb(nc, st, "cones%d" % layer, [128, 128], F32)
        lnt = sb(nc, st, "lnt%d" % layer, [128, 4, 32], F32)
        misc_sem = S_.new_sem()
        act_sem = S_.new_sem()

        t_ones = S_.op("dve", lambda e: e.memset(ones[:, :], 1.0))
        epst = sb(nc, st, "epst%d" % layer, [128, 1], F32)
        t_eps = S_.op("dve", lambda e: e.memset(epst[:, :], LN_EPS))
        t_lnp = S_.dma("sp", lambda e: e.dma_start(out=lnt[:, :, :], in_=lnp[:, :, :]), misc_sem)

        state = {
            "bank_free": [[] for _ in range(4)],
            "actA_free": [],
            "hid_free": [],
            "stat_free": [],
        }
        final_toks = []

        def run_gemm(src, kch, wd, ncols, src_ready, epilogue, mode_of=lambda c0: "fm"):
            lasts = []
            for g in range((ncols + 511) // 512):
                col0 = g * 512
                width = min(512, ncols - col0)
                if width == 512:
                    fw = [list(src_ready) + state["bank_free"][j] for j in range(4)]
                    last = gemm_group(S_, src, kch, wd, col0, wslots, wsem, pbanks, mode_of(col0), fw)
                    for j in range(4):
                        state["bank_free"][j] = [epilogue(g, j, pbanks[j], last[j])]
                    lasts = last
                else:
                    fw = list(src_ready) + state["bank_free"][0]
                    KB = wslots.tiles[0].shape[1]
                    wview = wd.rearrange("(kc p) n -> p kc n", p=128)
                    t = None
                    for kb in range(kch // KB):
                        k = wslots.next()
                        wt = wslots.tiles[k]
                        tokw = S_.dma("pool", lambda e, wt=wt, kb=kb: e.dma_start(
                            out=wt[:, :, 0:width], in_=wview[:, kb * KB:(kb + 1) * KB, col0:col0 + width]),
                            wsem[k], waits=wslots.free_tok[k])
                        for kk in range(KB):
                            kc = kb * KB + kk
                            w_ = ([tokw] if kk == 0 else []) + (fw if kc == 0 else [])
                            t = S_.op("pe", lambda e, kk=kk, kc=kc, wt=wt: e.matmul(
                                pbanks[0][0:width, :], lhsT=wt[:, kk, 0:width], rhs=src[:, kc, :],
                                start=(kc == 0), stop=(kc == kch - 1)), waits=w_)
                        wslots.free_tok[k] = [t]
                    state["bank_free"][0] = [epilogue(g, 0, pbanks[0], t)]
                    lasts = [t]
            return lasts

        def ln_gemm(src, kch, wd, src_ready, res_dram, t0_res, gi, bi, out_dram, t0_out, make_bf16):
            stat_w = list(state["stat_free"])
            ztoks = []

            def epi(g, j, bank, tlast):
                c = 4 * g + j
                kr = f32s["res"].next()
                rt = f32s["res"].tiles[kr]
                t_r = S_.dma("sp", lambda e, rt=rt, c=c: e.dma_start(
                    out=rt[:, :], in_=res_dram[c * 128:(c + 1) * 128, t0_res:t0_res + TT]),
                    f32sem["res"][kr], waits=f32s["res"].free_tok[kr])
                kz = f32s["zt"].next()
                zt = f32s["zt"].tiles[kz]
                t_z = S_.op("dve", lambda e, zt=zt, rt=rt, bank=bank: e.scalar_tensor_tensor(
                    out=zt[:, :], in0=rt[:, :], scalar=ALPHA, in1=bank[:, :], op0=ALU.mult, op1=ALU.add),
                    waits=[t_r, tlast] + f32s["zt"].free_tok[kz])
                f32s["res"].free_tok[kr] = [t_z]
                kq = f32s["zsq"].next()
                zq = f32s["zsq"].tiles[kq]
                t_q = S_.op("act", lambda e, zq=zq, zt=zt: e.activation(out=zq[:, :], in_=zt[:, :], func=AF.Square),
                            waits=[t_z] + f32s["zsq"].free_tok[kq])
                t_m1 = S_.op("pe", lambda e, zt=zt, c=c: e.matmul(st_sum[:, :], lhsT=ones[:, :], rhs=zt[:, :],
                                                                 start=(c == 0), stop=(c == 31)),
                             waits=[t_z, t_ones] + (stat_w if c == 0 else []))
                t_m2 = S_.op("pe", lambda e, zq=zq, c=c: e.matmul(st_sq[:, :], lhsT=ones[:, :], rhs=zq[:, :],
                                                                 start=(c == 0), stop=(c == 31)),
                             waits=[t_q])
                f32s["zsq"].free_tok[kq] = [t_m2]
                t_st = S_.dma("sp", lambda e, zt=zt, c=c: e.dma_start(
                    out=zbuf[c * 128:(c + 1) * 128, :], in_=zt[:, :]), f32sem["zt"][kz], waits=[t_z])
                f32s["zt"].free_tok[kz] = [t_st, t_m1]
                ztoks.append(t_st)
                epi.last_stat = t_m2
                return t_z

            lasts = run_gemm(src, kch, wd, D, src_ready, epi)
            src_done = lasts
            t_mean = S_.op("act", lambda e: e.activation(out=mean[:, :], in_=st_sum[:, :], func=AF.Copy, scale=1.0 / D),
                           waits=[epi.last_stat] + state.get("mean_free", []))
            t_msq = S_.op("dve", lambda e: e.tensor_tensor(out=rstd[:, :], in0=mean[:, :], in1=mean[:, :], op=ALU.mult),
                          waits=[t_mean] + state.get("mean_free", []))
            t_var = S_.op("dve", lambda e: e.scalar_tensor_tensor(
                out=rstd[:, :], in0=st_sq[:, :], scalar=1.0 / D, in1=rstd[:, :], op0=ALU.mult, op1=ALU.subtract),
                waits=[t_msq, epi.last_stat])
            t_ln = S_.op("act", lambda e: e.activation(out=rstd[:, :], in_=rstd[:, :], func=AF.Ln, bias=epst[:, 0:1]),
                         waits=[t_var, t_eps])
            t_rstd = S_.op("act", lambda e: e.activation(out=rstd[:, :], in_=rstd[:, :], func=AF.Exp, scale=-0.5),
                           waits=[t_ln])
            state["stat_free"] = [t_var, t_mean]
            users = []
            outs = []
            if make_bf16:
                S_.wait_only("act", src_done + state["actA_free"])
            for c in range(32):
                kz = f32s["zt"].next()
                zt = f32s["zt"].tiles[kz]
                t_l = S_.dma("sp", lambda e, zt=zt, c=c: e.dma_start(out=zt[:, :], in_=zbuf[c * 128:(c + 1) * 128, :]),
                             f32sem["zt"][kz], waits=f32s["zt"].free_tok[kz] + ztoks)
                kt = f32s["tmp"].next()
                tp = f32s["tmp"].tiles[kt]
                t_a = S_.op("dve", lambda e, tp=tp, zt=zt: e.tensor_tensor(out=tp[:, :], in0=zt[:, :], in1=mean[:, :],
                                                                       op=ALU.subtract),
                            waits=[t_l, t_mean, t_lnp] + f32s["tmp"].free_tok[kt])
                f32s["zt"].free_tok[kz] = [t_a]
                t_b = S_.op("dve", lambda e, tp=tp: e.tensor_tensor(out=tp[:, :], in0=tp[:, :], in1=rstd[:, :], op=ALU.mult),
                            waits=[t_a, t_rstd])
                t_c = S_.op("act", lambda e, tp=tp, c=c: e.activation(
                    out=tp[:, :], in_=tp[:, :], func=AF.Identity, scale=lnt[:, gi, c:c + 1], bias=lnt[:, bi, c:c + 1]),
                    waits=[t_b])
                t_o = S_.dma("sp", lambda e, tp=tp, c=c: e.dma_start(
                    out=out_dram[c * 128:(c + 1) * 128, t0_out:t0_out + TT], in_=tp[:, :]),
                    f32sem["tmp"][kt], waits=[t_c])
                fr = [t_o]
                if make_bf16:
                    t_h = S_.op("act", lambda e, tp=tp, c=c: e.activation(out=actA[:, c, :], in_=tp[:, :], func=AF.Copy),
                                waits=[t_c])
                    fr.append(t_h)
                    outs.append(t_h)
                f32s["tmp"].free_tok[kt] = fr
                users += [t_a, t_b]
                final_toks.append(t_o)
            state["mean_free"] = users[-2:]
            return outs, src_done

        GC = 1.5957691216057308
        for tt in range(T // TT):
            t0 = tt * TT
            if layer == 0:
                sb_toks = []
                for cc in range(16):
                    sb_toks.append(S_.dma("sp", lambda e, t0=t0, cc=cc: e.dma_start(
                        out=actA[:, 16 + cc, :], in_=io["ysb_src"](e, cc, t0)), misc_sem, waits=state["actA_free"]))
                t_sb = sb_toks[-1]
                gT = hid
                gtoks = []
                for c in range(16):
                    kr = f32s["res"].next()
                    xt = f32s["res"].tiles[kr]
                    t_x = S_.dma("sp", lambda e, xt=xt, c=c, t0=t0: e.dma_start(
                        out=xt[:, :], in_=io["yssm_src"](e, c, t0)),
                        f32sem["res"][kr], waits=f32s["res"].free_tok[kr])
                    ka = f32s["tmp2"].next()
                    ta = f32s["tmp2"].tiles[ka]
                    t1 = S_.op("dve", lambda e, ta=ta, xt=xt: e.tensor_tensor(out=ta[:, :], in0=xt[:, :], in1=xt[:, :], op=ALU.mult),
                               waits=[t_x] + f32s["tmp2"].free_tok[ka])
                    t2 = S_.op("dve", lambda e, ta=ta: e.tensor_scalar(out=ta[:, :], in0=ta[:, :], scalar1=0.044715, scalar2=1.0,
                                                                      op0=ALU.mult, op1=ALU.add), waits=[t1])
                    t3 = S_.op("dve", lambda e, ta=ta, xt=xt: e.tensor_tensor(out=ta[:, :], in0=ta[:, :], in1=xt[:, :], op=ALU.mult),
                               waits=[t2])
                    t4 = S_.op("act", lambda e, ta=ta: e.activation(out=ta[:, :], in_=ta[:, :], func=AF.Sigmoid, scale=GC),
                               waits=[t3])
                    t5 = S_.op("dve", lambda e, ta=ta, xt=xt, c=c: e.tensor_tensor(out=gT[:, c, :], in0=ta[:, :], in1=xt[:, :], op=ALU.mult),
                               waits=[t4] + state["hid_free"])
                    f32s["res"].free_tok[kr] = [t5]
                    f32s["tmp2"].free_tok[ka] = [t5]
                    gtoks.append(t5)

                def epi_glu(g, j, bank, tlast):
                    c = 4 * g + j
                    ka = f32s["tmp2"].next()
                    ta = f32s["tmp2"].tiles[ka]
                    t_s = S_.op("act", lambda e, ta=ta, bank=bank: e.activation(out=ta[:, :], in_=bank[:, :], func=AF.Sigmoid),
                                waits=[tlast] + f32s["tmp2"].free_tok[ka])
                    t_g = S_.op("dve", lambda e, ta=ta, c=c: e.tensor_tensor(out=actA[:, c, :], in0=ta[:, :], in1=gT[:, c, :], op=ALU.mult),
                                waits=[t_s] + state["actA_free"])
                    f32s["tmp2"].free_tok[ka] = [t_g]
                    epi_glu.toks.append(t_g)
                    return t_s
                epi_glu.toks = []
                lasts = run_gemm(gT, 16, w_glu, 2048, gtoks, epi_glu)
                cat_ready = epi_glu.toks + [t_sb]
                state["hid_free"] = list(lasts) + epi_glu.toks
            else:
                y_toks = []
                for cc in range(32):
                    y_toks.append(S_.dma("sp", lambda e, t0=t0, cc=cc: e.dma_start(
                        out=actA[:, cc, :], in_=io["yin_src"](e, cc, t0)), misc_sem, waits=state["actA_free"]))
                cat_ready = [y_toks[-1]]
            houts, src_done = ln_gemm(actA, 32, w_out, cat_ready, hres, t0, 0, 1, hmid, 0, True)
            state["actA_free"] = []
            def epi_relu2(g, j, bank, tlast):
                c = 4 * g + j
                ka = f32s["tmp2"].next()
                ta = f32s["tmp2"].tiles[ka]
                t_r = S_.op("act", lambda e, ta=ta, bank=bank: e.activation(out=ta[:, :], in_=bank[:, :], func=AF.Relu),
                            waits=[tlast] + f32s["tmp2"].free_tok[ka])
                t_h = S_.op("pool", lambda e, ta=ta, c=c: e.tensor_tensor(out=hid[:, c, :], in0=ta[:, :], in1=ta[:, :], op=ALU.mult),
                            waits=[t_r] + state["hid_free"])
                f32s["tmp2"].free_tok[ka] = [t_h]
                epi_relu2.toks.append(t_h)
                return t_r
            epi_relu2.toks = []
            lasts1 = run_gemm(actA, 32, w1, 4 * D, houts, epi_relu2)
            state["actA_free"] = list(lasts1)
            hm_toks = final_toks[-32:]
            S_.wait_only("sp", hm_toks)
            houts2, src_done2 = ln_gemm(hid, 128, w2, epi_relu2.toks, hmid, 0, 2, 3, hout, t0, layer == 0)
            state["hid_free"] = list(src_done2)
            if layer == 0:
                def epi_store(g, j, bank, tlast):
                    col0 = g * 512
                    if col0 >= 3 * D:
                        kt = f32s["tmp2"].next()
                        tp = f32s["tmp2"].tiles[kt]
                        t_e = S_.op("act", lambda e, tp=tp, bank=bank: e.activation(out=tp[0:32, :], in_=bank[0:32, :], func=AF.Copy),
                                    waits=[tlast] + f32s["tmp2"].free_tok[kt])
                        t_d = S_.dma("sp", lambda e, tp=tp, tt=tt: e.dma_start(out=io["f_dst"](tt), in_=tp[0:32, :]),
                                     act_sem, waits=[t_e])
                        fox_toks.append(t_d)
                        f32s["tmp2"].free_tok[kt] = [t_d]
                        final_toks.append(t_d)
                        return t_e
                    ko = o16.next()
                    ot = o16.tiles[ko]
                    if j % 2 == 0:
                        t_e = S_.op("act", lambda e, ot=ot, bank=bank: e.activation(out=ot[:, :], in_=bank[:, :], func=AF.Copy),
                                    waits=[tlast] + o16.free_tok[ko])
                    else:
                        t_e = S_.op("dve", lambda e, ot=ot, bank=bank: e.tensor_copy(out=ot[:, :], in_=bank[:, :]),
                                    waits=[tlast] + o16.free_tok[ko])
                    dv = io["fox_dst"](col0, j, tt)
                    t_d = S_.dma("sp", lambda e, ot=ot, dv=dv: e.dma_start(out=dv, in_=ot[:, :]), o16sem[ko], waits=[t_e])
                    fox_toks.append(t_d)
                    o16.free_tok[ko] = [t_d]
                    final_toks.append(t_d)
                    return t_e
                fox_toks = []
                lasts3 = run_gemm(actA, 32, w_fox, FOX_IN, houts2, epi_store,
                                  mode_of=lambda c0: "tm" if c0 >= 2 * D else "fm")
                state["actA_free"] = list(lasts3)
                io["c_after_tile"](tt, fox_toks)
        S_.wait_only("sp", [(sm, S_.dsem_cnt[id(sm)]) for sm in f32sem["tmp"] + o16sem + [act_sem]])


SCALE = 128 ** -0.5


def emit_fox(S_, nc, st, DR, io, NH=16, SEQ=S):
    NJ = SEQ // 128
    NI = SEQ // 512
    bf, ident_d, tri_d = DR["bf"], DR["ident"], DR["tri"]
    if True:
        ident = sb(nc, st, "identt", [128, 128], F32)
        tri = sb(nc, st, "trit", [128, 128], BF16)
        sel0 = sb(nc, st, "sel0", [128, 128], F32)
        ones_bf = sb(nc, st, "ones_bf", [128, 128], BF16)
        onesf = sb(nc, st, "onesf", [NH, SEQ], F32)
        fl = sb(nc, st, "fl", [NH, SEQ], F32)
        cneg = sb(nc, st, "cneg", [NH, SEQ], F32)
        nb = sb(nc, st, "nb", [NH, 1], F32)
        one1 = sb(nc, st, "one1", [NH, 1], F32)
        cnegT = sb(nc, st, "cnegT", [128, NJ, NH], F32)
        cq = sb(nc, st, "cq", [128, NI, NH], F32)
        biasT = sb(nc, st, "biasT", [128, NH, NJ, NI], F32)
        kt_s = Slots([sb(nc, st, "kt%d" % i, [128, SEQ], BF16) for i in range(2)])
        v_s = Slots([sb(nc, st, "vt%d" % i, [128, NJ, 128], BF16) for i in range(2)])
        q_s = Slots([sb(nc, st, "qt%d" % i, [128, 512], BF16) for i in range(3)])
        p_s = Slots([sb(nc, st, "pt%d" % i, [128, 512], BF16) for i in range(6)])
        y_s = Slots([sb(nc, st, "yt%d" % i, [128, 512], BF16) for i in range(2)])
        rinv = Slots([sb(nc, st, "rinv%d" % i, [128, 512], F32) for i in range(2)])
        sbank = [ps(nc, st, "sbk%d" % i, [128, 512]) for i in range(3)]
        obank = [ps(nc, st, "obk%d" % i, [128, 512]) for i in range(2)]
        rbank = [ps(nc, st, "rbk%d" % i, [128, 512]) for i in range(2)]
        tbank = ps(nc, st, "tbk", [128, 512])
        msem = S_.new_sem()
        ksem = [S_.new_sem() for _ in range(2)]
        vsem = [S_.new_sem() for _ in range(2)]
        qsem = [S_.new_sem() for _ in range(3)]
        ysem = [S_.new_sem() for _ in range(2)]

        d_id = S_.dma("sp", lambda e: e.dma_start(out=ident[:, :], in_=ident_d[:, :]), msem)
        d_tri = S_.dma("sp", lambda e: e.dma_start(out=tri[:, :], in_=tri_d[:, :]), msem)
        d_f = io["load_f"](S_, fl, msem)
        d_b = S_.dma("sp", lambda e: e.dma_start(out=nb[:, :], in_=bf[:, :]), msem)
        m1 = S_.op("dve", lambda e: e.memset(sel0[:, :], 0.0))
        m2 = S_.op("dve", lambda e: e.memset(sel0[0:1, :], 1.0), waits=[m1])
        m3 = S_.op("dve", lambda e: e.memset(ones_bf[:, :], 1.0))
        m4 = S_.op("dve", lambda e: e.memset(onesf[:, :], 1.0))
        m5 = S_.op("dve", lambda e: e.memset(one1[:, :], 1.0))
        m6 = S_.op("dve", lambda e: e.tensor_scalar(out=nb[:, :], in0=nb[:, :], scalar1=-1.0, scalar2=None, op0=ALU.mult),
                   waits=[d_b])
        a1 = S_.op("act", lambda e: e.activation(out=fl[:, :], in_=fl[:, :], func=AF.Exp, scale=-1.0, bias=nb[:, 0:1]),
                   waits=[d_f, m6])
        a2 = S_.op("act", lambda e: e.activation(out=fl[:, :], in_=fl[:, :], func=AF.Ln, bias=one1[:, 0:1]), waits=[a1, m5])
        sc = S_.op("dve", lambda e: e.tensor_tensor_scan(out=cneg[:, :], data0=onesf[:, :], data1=fl[:, :], initial=0.0,
                                                        op0=ALU.mult, op1=ALU.add), waits=[a2, m4])
        tps = []
        for j in range(NJ):
            tps.append(S_.op("pe", lambda e, j=j: e.transpose(out=tbank[:, j * NH:(j + 1) * NH],
                                                              in_=cneg[0:NH, j * 128:(j + 1) * 128],
                                                              identity=ident[0:NH, 0:NH]), waits=[sc, d_id]))
        c1 = S_.op("dve", lambda e: e.tensor_copy(out=cnegT[:, :, :], in_=tbank[:, 0:NJ * NH].rearrange("p (j h) -> p j h", h=NH)),
                   waits=[tps[-1]])
        mm = S_.op("pe", lambda e: e.matmul(tbank[:, 0:NI * NH], lhsT=sel0[:, :], rhs=cnegT[:, 2::4, :], start=True, stop=True),
                   waits=[c1, m2])
        c2 = S_.op("dve", lambda e: e.tensor_copy(out=cq[:, :, :], in_=tbank[:, 0:NI * NH].rearrange("p (i h) -> p i h", h=NH)),
                   waits=[mm])
        bt = c2
        for h in range(NH):
            for j in range(NJ):
                bt = S_.op("dve", lambda e, h=h, j=j: e.tensor_scalar(
                    out=biasT[:, h, j, :], in0=cq[:, :, h], scalar1=cnegT[:, j, h:h + 1], scalar2=-1.0,
                    op0=ALU.subtract, op1=ALU.mult), waits=[c2])
        bias_ready = bt

        NSB = len(sbank)
        sfree = [[] for _ in range(NSB)]
        ofree = [[] for _ in range(2)]
        its = []
        heads = {}
        groups = {}
        for h in range(NH):
            for I in range(NI):
                nj = 4 * I + 4
                for j in range(nj):
                    its.append({"h": h, "I": I, "j": j, "nj": nj})

        def head_ctx(h):
            if h not in heads:
                kk = kt_s.next()
                ktile = kt_s.tiles[kk]
                d_k = io["load_k"](S_, h, ktile, ksem[kk], kt_s.free_tok[kk])
                kv = v_s.next()
                vtile = v_s.tiles[kv]
                d_v = io["load_v"](S_, h, vtile, vsem[kv], v_s.free_tok[kv])
                heads[h] = {"kk": kk, "ktile": ktile, "d_k": d_k, "kv": kv, "vtile": vtile, "d_v": d_v, "toks": []}
            return heads[h]

        def group_ctx(h, I):
            if (h, I) not in groups:
                kq = q_s.next()
                qtile = q_s.tiles[kq]
                d_q = io["load_q"](S_, h, I, qtile, qsem[kq], q_s.free_tok[kq])
                groups[(h, I)] = {"kq": kq, "qtile": qtile, "d_q": d_q, "ob": (h * NI + I) % 2}
            return groups[(h, I)]

        def stage1(n, it):
            h, I, j = it["h"], it["I"], it["j"]
            hc = head_ctx(h)
            gc = group_ctx(h, I)
            r = j - 4 * I
            q0 = 128 * r if r > 0 else 0
            sbk = n % NSB
            ktile, qtile = hc["ktile"], gc["qtile"]
            t_s = S_.op("pe", lambda e: e.matmul(
                sbank[sbk][:, q0:512], lhsT=ktile[:, j * 128:(j + 1) * 128], rhs=qtile[:, q0:512],
                start=True, stop=True), waits=[hc["d_k"], gc["d_q"]] + sfree[sbk])
            kp = p_s.next()
            pt = p_s.tiles[kp]
            t_e = S_.op("act", lambda e: e.activation(
                out=pt[:, q0:512], in_=sbank[sbk][:, q0:512], func=AF.Exp, scale=SCALE,
                bias=biasT[:, h, j, I:I + 1]), waits=[t_s, bias_ready] + p_s.free_tok[kp])
            sfree[sbk] = [t_e]
            t_p = t_e
            if r >= 0:
                t_p = S_.op("pool", lambda e: e.tensor_tensor(
                    out=pt[:, q0:q0 + 128], in0=pt[:, q0:q0 + 128], in1=tri[:, :], op=ALU.mult),
                    waits=[t_e, d_tri])
            it.update({"q0": q0, "kp": kp, "pt": pt, "t_p": t_p, "t_s": t_s})

        def stage2(n, it):
            h, I, j, nj = it["h"], it["I"], it["j"], it["nj"]
            hc = heads[h]
            gc = groups[(h, I)]
            ob, q0, pt, kp = gc["ob"], it["q0"], it["pt"], it["kp"]
            vtile = hc["vtile"]
            t_o = S_.op("pe", lambda e: e.matmul(
                obank[ob][:, q0:512], lhsT=vtile[:, j, :], rhs=pt[:, q0:512], start=(j == 0), stop=(j == nj - 1)),
                waits=[it["t_p"], hc["d_v"]] + (ofree[ob] if j == 0 else []))
            t_r = S_.op("pe", lambda e: e.matmul(
                rbank[ob][:, q0:512], lhsT=ones_bf[:, :], rhs=pt[:, q0:512], start=(j == 0), stop=(j == nj - 1)),
                waits=[m3])
            p_s.free_tok[kp] = [t_r]
            if j == nj - 1:
                q_s.free_tok[gc["kq"]] = [t_r]
                kr = rinv.next()
                rt = rinv.tiles[kr]
                t_ri = S_.op("dve", lambda e: e.reciprocal(out=rt[:, :], in_=rbank[ob][:, :]),
                             waits=[t_r] + rinv.free_tok[kr])
                ky = y_s.next()
                yt = y_s.tiles[ky]
                t_y = S_.op("dve", lambda e: e.tensor_tensor(
                    out=yt[:, :], in0=obank[ob][:, :], in1=rt[:, :], op=ALU.mult), waits=[t_ri] + y_s.free_tok[ky])
                rinv.free_tok[kr] = [t_y]
                ofree[ob] = [t_y]
                d_y = S_.dma("sp", lambda e: e.dma_start(out=io["y_dst"](h, I), in_=yt[:, :]), ysem[ky], waits=[t_y])
                y_s.free_tok[ky] = [d_y]
                hc["toks"].append(d_y)
                if I == NI - 1:
                    kt_s.free_tok[hc["kk"]] = [t_r]
                    v_s.free_tok[hc["kv"]] = [t_r]
                    io["after_head"](h, hc["toks"])

        SK = NSB - 1
        N = len(its)
        for step in range(N + SK):
            if step < N:
                stage1(step, its[step])
            if step - SK >= 0:
                stage2(step - SK, its[step - SK])
        S_.wait_only("sp", [(sm, S_.dsem_cnt[id(sm)]) for sm in ysem])


def emit_sb(S_, nc, st, tri_d, u_d, io, NH, SEQ, final_sems):
    NJ = SEQ // 128
    NI = SEQ // 512
    tri = sb(nc, st, "sb_tri", [128, 128], BF16)
    umat = sb(nc, st, "sb_u", [128, 128], BF16)
    ones_bf = sb(nc, st, "sb_ones", [128, 128], BF16)
    one1 = sb(nc, st, "sb_one1", [128, 1], F32)
    kt_s = Slots([sb(nc, st, "sb_kt%d" % i, [128, SEQ], BF16) for i in range(2)])
    v_s = Slots([sb(nc, st, "sb_vt%d" % i, [128, NJ, 128], BF16) for i in range(2)])
    q_s = Slots([sb(nc, st, "sb_qt%d" % i, [128, 512], BF16) for i in range(3)])
    l_s = Slots([sb(nc, st, "sb_l%d" % i, [128, 512], F32) for i in range(4)])
    lw_s = Slots([sb(nc, st, "sb_lw%d" % i, [128, 512], F32) for i in range(2)])
    lk_s = Slots([sb(nc, st, "sb_lk%d" % i, [128, 512], BF16) for i in range(4)])
    w_s = Slots([sb(nc, st, "sb_w%d" % i, [128, 512], BF16) for i in range(4)])
    y_s = Slots([sb(nc, st, "sb_y%d" % i, [128, 512], BF16) for i in range(2)])
    a_s = Slots([sb(nc, st, "sb_a%d" % i, [128, 512], BF16) for i in range(2)])
    zbank = [ps(nc, st, "sb_zb%d" % i, [128, 512]) for i in range(3)]
    lbank = [ps(nc, st, "sb_lb%d" % i, [128, 512]) for i in range(2)]
    obank = [ps(nc, st, "sb_ob%d" % i, [128, 512]) for i in range(2)]
    msem = S_.new_sem()
    ksem = [S_.new_sem() for _ in range(2)]
    vsem = [S_.new_sem() for _ in range(2)]
    qsem = [S_.new_sem() for _ in range(3)]
    ysem = [S_.new_sem() for _ in range(2)]
    final_sems += ysem
    d_tri = S_.dma("sp", lambda e: e.dma_start(out=tri[:, :], in_=tri_d[:, :]), msem)
    d_u = S_.dma("sp", lambda e: e.dma_start(out=umat[:, :], in_=u_d[:, :]), msem)
    m3 = S_.op("dve", lambda e: e.memset(ones_bf[:, :], 1.0))
    m5 = S_.op("dve", lambda e: e.memset(one1[:, :], 1.0))
    NZB = len(zbank)
    zfree = [[] for _ in range(NZB)]
    lfree = [[] for _ in range(2)]
    ofree = [[] for _ in range(2)]
    its = []
    heads = {}
    groups = {}
    for h in range(NH):
        for I in range(NI):
            jtop = 4 * I + 3
            for j in range(jtop, -1, -1):
                its.append({"h": h, "I": I, "j": j, "jtop": jtop})

    def head_ctx(h):
        if h not in heads:
            kk = kt_s.next()
            ktile = kt_s.tiles[kk]
            d_k = io["load_k"](S_, h, ktile, ksem[kk], kt_s.free_tok[kk])
            kv = v_s.next()
            vtile = v_s.tiles[kv]
            d_v = io["load_v"](S_, h, vtile, vsem[kv], v_s.free_tok[kv])
            heads[h] = {"kk": kk, "ktile": ktile, "d_k": d_k, "kv": kv, "vtile": vtile, "d_v": d_v, "toks": []}
        return heads[h]

    def group_ctx(h, I):
        if (h, I) not in groups:
            kq = q_s.next()
            qtile = q_s.tiles[kq]
            d_q = io["load_q"](S_, h, I, qtile, qsem[kq], q_s.free_tok[kq])
            groups[(h, I)] = {"kq": kq, "qtile": qtile, "d_q": d_q, "ob": (h * NI + I) % 2}
        return groups[(h, I)]

    def stage1(n, it):
        h, I, j = it["h"], it["I"], it["j"]
        hc = head_ctx(h)
        gc = group_ctx(h, I)
        r = j - 4 * I
        q0 = 128 * r if r > 0 else 0
        zb = n % NZB
        ktile, qtile = hc["ktile"], gc["qtile"]
        t_z = S_.op("pe", lambda e: e.matmul(
            zbank[zb][:, q0:512], lhsT=ktile[:, j * 128:(j + 1) * 128], rhs=qtile[:, q0:512],
            start=True, stop=True), waits=[hc["d_k"], gc["d_q"]] + zfree[zb])
        kl = l_s.next()
        lt = l_s.tiles[kl]
        t_e = S_.op("act", lambda e: e.activation(
            out=lt[:, q0:512], in_=zbank[zb][:, q0:512], func=AF.Exp, scale=-SCALE),
            waits=[t_z] + l_s.free_tok[kl])
        t_l = S_.op("act", lambda e: e.activation(
            out=lt[:, q0:512], in_=lt[:, q0:512], func=AF.Ln, bias=one1[:, 0:1]), waits=[t_e, m5])
        kk2 = lk_s.next()
        lkt = lk_s.tiles[kk2]
        t_lk = S_.op("dve", lambda e: e.scalar_tensor_tensor(
            out=lkt[:, q0:512], in0=zbank[zb][:, q0:512], scalar=-SCALE, in1=lt[:, q0:512],
            op0=ALU.mult, op1=ALU.subtract), waits=[t_l] + lk_s.free_tok[kk2])
        zfree[zb] = [t_lk]
        if r >= 0:
            t_lk = S_.op("pool", lambda e: e.tensor_tensor(
                out=lkt[:, q0:q0 + 128], in0=lkt[:, q0:q0 + 128], in1=tri[:, :], op=ALU.mult),
                waits=[t_lk, d_tri])
        it.update({"r": r, "q0": q0, "kl": kl, "lt": lt, "t_l": t_l, "kk2": kk2, "lkt": lkt, "t_lk": t_lk})

    def stage2(n, it):
        h, I, j, jtop = it["h"], it["I"], it["j"], it["jtop"]
        gc = groups[(h, I)]
        r, q0, lt, lkt, t_lk, t_l = it["r"], it["q0"], it["lt"], it["lkt"], it["t_lk"], it["t_l"]
        first = (j == jtop)
        if first:
            ka = a_s.next()
            at = a_s.tiles[ka]
            gc["ka"] = ka
            gc["at"] = at
            gc["t_a"] = S_.op("pool", lambda e: e.memset(at[:, :], 0.0), waits=a_s.free_tok[ka])
        at = gc["at"]
        lb = n % 2
        t_L = None
        if not first:
            t_L = S_.op("pe", lambda e: e.matmul(
                lbank[lb][:, q0:512], lhsT=ones_bf[:, :], rhs=at[:, q0:512], start=True, stop=False),
                waits=[gc["t_a"], m3] + lfree[lb])
        t_L2 = S_.op("pe", lambda e: e.matmul(
            lbank[lb][:, q0:512], lhsT=umat[:, :], rhs=lkt[:, q0:512], start=first, stop=True),
            waits=[t_lk, d_u] + (lfree[lb] if first else []))
        if j > 0:
            gc["t_a"] = S_.op("pool", lambda e: e.tensor_tensor(
                out=at[:, q0:512], in0=at[:, q0:512], in1=lkt[:, q0:512], op=ALU.add),
                waits=[t_lk, gc["t_a"]] + ([t_L] if t_L is not None else []))
        gc["a_last_read"] = t_L if t_L is not None else gc.get("a_last_read")
        klw = lw_s.next()
        lwt = lw_s.tiles[klw]
        t_lw = S_.op("dve", lambda e: e.tensor_tensor(
            out=lwt[:, q0:512], in0=lbank[lb][:, q0:512], in1=lt[:, q0:512], op=ALU.subtract),
            waits=[t_L2, t_l] + lw_s.free_tok[klw])
        lfree[lb] = [t_lw]
        l_s.free_tok[it["kl"]] = [t_lw]
        kw = w_s.next()
        wt = w_s.tiles[kw]
        t_w = S_.op("act", lambda e: e.activation(
            out=wt[:, q0:512], in_=lwt[:, q0:512], func=AF.Exp), waits=[t_lw] + w_s.free_tok[kw])
        lw_s.free_tok[klw] = [t_w]
        if r >= 0:
            t_w = S_.op("pool", lambda e: e.tensor_tensor(
                out=wt[:, q0:q0 + 128], in0=wt[:, q0:q0 + 128], in1=tri[:, :], op=ALU.mult), waits=[t_w, d_tri])
        lk_s.free_tok[it["kk2"]] = [t_L2, gc["t_a"]]
        it.update({"kw": kw, "wt": wt, "t_w": t_w, "first": first})

    def stage3(n, it):
        h, I, j = it["h"], it["I"], it["j"]
        hc = heads[h]
        gc = groups[(h, I)]
        ob, q0, wt, first = gc["ob"], it["q0"], it["wt"], it["first"]
        vtile = hc["vtile"]
        t_o = S_.op("pe", lambda e: e.matmul(
            obank[ob][:, q0:512], lhsT=vtile[:, j, :], rhs=wt[:, q0:512], start=first, stop=(j == 0)),
            waits=[it["t_w"], hc["d_v"]] + (ofree[ob] if first else []))
        w_s.free_tok[it["kw"]] = [t_o]
        if j == 0:
            a_s.free_tok[gc["ka"]] = [gc["t_a"], t_o] + ([gc["a_last_read"]] if gc.get("a_last_read") else [])
            q_s.free_tok[gc["kq"]] = [t_o]
            ky = y_s.next()
            yt = y_s.tiles[ky]
            t_y = S_.op("dve", lambda e: e.tensor_copy(out=yt[:, :], in_=obank[ob][:, :]),
                        waits=[t_o] + y_s.free_tok[ky])
            ofree[ob] = [t_y]
            d_y = S_.dma("sp", lambda e: e.dma_start(out=io["y_dst"](h, I), in_=yt[:, :]), ysem[ky], waits=[t_y])
            y_s.free_tok[ky] = [d_y]
            hc["toks"].append(d_y)
            if I == NI - 1:
                kt_s.free_tok[hc["kk"]] = [t_o]
                v_s.free_tok[hc["kv"]] = [t_o]
                io["after_head"](h, hc["toks"])

    N = len(its)
    for step in range(N + 2):
        if step < N:
            stage1(step, its[step])
        if 0 <= step - 1 < N:
            stage2(step - 1, its[step - 1])
        if 0 <= step - 2 < N:
            stage3(step - 2, its[step - 2])
        if "tick" in io:
            io["tick"]()


PI = float(np.pi)
TWO_PI = float(2 * np.pi)
MAGIC = 12582912.0


def emit_s5(S_, nc, st, DR, io, NG=64, SEQ=S):
    NT = SEQ // 512
    wv_d, wsw_d, x1_d, x2_d, lr_d, li_d, ldt_d, d16_d, sgn_d, iota_d = (
        DR[k] for k in ("wv", "wsw", "x1", "x2", "lr", "li", "ldt", "d16", "sgn", "iota"))
    if True:
        big = lambda n, d=F32: sb(nc, st, n, [128, SEQ], d)
        iota = big("iota_t")
        mt = big("mt")
        s2 = big("s2")
        c2 = big("c2")
        magf = big("magf")
        btl = big("btl")
        xs = big("xs")
        pp = big("pp", BF16)
        qq = big("qq", BF16)
        ub_s = Slots([big("ub%d" % i, BF16) for i in range(2)])
        wv_s = Slots([sb(nc, st, "wv%d" % i, [128, 8, 128], BF16) for i in range(2)])
        ws_s = Slots([sb(nc, st, "ws%d" % i, [128, 8, 128], BF16) for i in range(2)])
        x1 = sb(nc, st, "x1t", [128, NG, 16], F32)
        x2 = sb(nc, st, "x2t", [128, NG, 16], F32)
        sm = {n: sb(nc, st, "p_" + n, [128, NG], F32) for n in
              ("lr", "li", "dt", "mag", "th", "t1", "t2", "cs", "sn", "zr", "zi", "den", "zrs", "nzi", "nzrs", "nr")}
        d16 = sb(nc, st, "d16t", [16, NG], F32)
        sgn = sb(nc, st, "sgnt", [128, 1], F32)
        nsgnpi = sb(nc, st, "nsgnpi", [128, 1], F32)
        negpi = sb(nc, st, "negpi", [128, 1], F32)
        w1g = Slots([sb(nc, st, "w1g%d" % i, [128, 16], BF16) for i in range(2)])
        w2g = Slots([sb(nc, st, "w2g%d" % i, [128, 16], BF16) for i in range(2)])
        wtmp = Slots([sb(nc, st, "wtmp%d" % i, [128, 16], F32) for i in range(2)])
        t1_s = Slots([sb(nc, st, "t1_%d" % i, [128, 512], F32) for i in range(2)])
        t2_s = Slots([sb(nc, st, "t2_%d" % i, [128, 512], F32) for i in range(2)])
        u32_s = Slots([sb(nc, st, "u32_%d" % i, [16, 512], F32) for i in range(3)])
        yo_s = Slots([sb(nc, st, "yo_%d" % i, [16, 512], F32) for i in range(3)])
        vbank = [ps(nc, st, "vb%d" % i, [128, 512]) for i in range(2)]
        wbank = [ps(nc, st, "wb%d" % i, [128, 512]) for i in range(2)]
        ybank = [ps(nc, st, "yb%d" % i, [128, 512]) for i in range(2)]
        msem = S_.new_sem()
        ubsem = [S_.new_sem() for _ in range(2)]
        wvsem = [S_.new_sem() for _ in range(2)]
        wssem = [S_.new_sem() for _ in range(2)]
        u32sem = [S_.new_sem() for _ in range(3)]
        yosem = [S_.new_sem() for _ in range(3)]

        ld = {}
        for nm, dst, src in (("iota", iota, iota_d), ("x1", x1, x1_d), ("x2", x2, x2_d), ("lr", sm["lr"], lr_d),
                             ("li", sm["li"], li_d), ("dt", sm["dt"], ldt_d), ("d16", d16, d16_d), ("sgn", sgn, sgn_d)):
            if len(src.shape) == 3:
                ld[nm] = S_.dma("sp", lambda e, dst=dst, src=src: e.dma_start(out=dst[:, :, :], in_=src[:, :, :]), msem)
            else:
                ld[nm] = S_.dma("sp", lambda e, dst=dst, src=src: e.dma_start(out=dst[:, :], in_=src[:, :]), msem)
        A = lambda n: sm[n][:, :]
        tt_ = lambda o, a, b, op, w: S_.op("dve", lambda e: e.tensor_tensor(out=A(o), in0=A(a), in1=A(b), op=op), waits=w)
        ts_ = lambda o, a, s1, s2_, op0, op1, w: S_.op("dve", lambda e: e.tensor_scalar(
            out=A(o), in0=A(a), scalar1=s1, scalar2=s2_, op0=op0, op1=op1), waits=w)
        k0 = S_.op("dve", lambda e: e.tensor_scalar(out=nsgnpi[:, :], in0=sgn[:, :], scalar1=-TWO_PI, scalar2=None, op0=ALU.mult),
                   waits=[ld["sgn"]])
        k1 = S_.op("dve", lambda e: e.memset(negpi[:, :], -0.5 * PI))
        hpi = sb(nc, st, "hpi", [128, 1], F32)
        k1 = S_.op("dve", lambda e: e.memset(hpi[:, :], 0.5 * PI), waits=[k1])
        magt = sb(nc, st, "magt", [128, 1], F32)
        k1 = S_.op("dve", lambda e: e.memset(magt[:, :], MAGIC), waits=[k1])
        a_dt = S_.op("act", lambda e: e.activation(out=A("dt"), in_=A("dt"), func=AF.Exp), waits=[ld["dt"]])
        p1 = tt_("t1", "lr", "dt", ALU.mult, [a_dt, ld["lr"]])
        a_mag = S_.op("act", lambda e: e.activation(out=A("mag"), in_=A("t1"), func=AF.Exp), waits=[p1])
        p2 = tt_("th", "li", "dt", ALU.mult, [a_dt, ld["li"]])
        p3a = ts_("t2", "th", 1.0 / TWO_PI, None, ALU.mult, ALU.bypass, [p2])
        p3b = ts_("cs", "t2", MAGIC, None, ALU.add, ALU.bypass, [p3a])
        p3c = ts_("cs", "cs", -MAGIC, None, ALU.add, ALU.bypass, [p3b])
        p3 = tt_("th", "t2", "cs", ALU.subtract, [p3c])
        a_sn = S_.op("act", lambda e: e.activation(out=A("sn"), in_=A("th"), func=AF.Sin, scale=TWO_PI), waits=[p3])
        a_ab = S_.op("act", lambda e: e.activation(out=A("t2"), in_=A("th"), func=AF.Abs), waits=[p3, p3a])
        a_cs = S_.op("act", lambda e: e.activation(out=A("cs"), in_=A("t2"), func=AF.Sin, scale=-TWO_PI, bias=hpi[:, 0:1]),
                     waits=[a_ab, k1, p3])
        p5 = tt_("cs", "cs", "mag", ALU.mult, [a_cs, a_mag])
        p6 = ts_("nr", "cs", -1.0, None, ALU.add, ALU.bypass, [p5])
        p8 = tt_("sn", "sn", "mag", ALU.mult, [a_sn, a_mag])
        p9 = tt_("den", "lr", "lr", ALU.mult, [ld["lr"]])
        p10 = tt_("t1", "li", "li", ALU.mult, [ld["li"], a_mag])
        p11 = tt_("den", "den", "t1", ALU.add, [p9, p10])
        p12 = S_.op("dve", lambda e: e.reciprocal(out=A("den"), in_=A("den")), waits=[p11])
        p13 = tt_("zr", "nr", "lr", ALU.mult, [p6])
        p14 = tt_("t1", "sn", "li", ALU.mult, [p8, p11])
        p15 = tt_("zr", "zr", "t1", ALU.add, [p13, p14])
        p16 = tt_("zr", "zr", "den", ALU.mult, [p15, p12])
        p17 = tt_("zi", "sn", "lr", ALU.mult, [p8])
        p18 = tt_("t2", "nr", "li", ALU.mult, [p6, a_cs])
        p19 = tt_("zi", "zi", "t2", ALU.subtract, [p17, p18])
        p20 = tt_("zi", "zi", "den", ALU.mult, [p19, p12])
        p21 = S_.op("dve", lambda e: e.tensor_scalar(out=A("zrs"), in0=A("zr"), scalar1=sgn[:, 0:1], scalar2=None, op0=ALU.mult),
                    waits=[p16, ld["sgn"]])
        p22 = ts_("nzi", "zi", -1.0, None, ALU.mult, ALU.bypass, [p20])
        p23 = ts_("nzrs", "zrs", -1.0, None, ALU.mult, ALU.bypass, [p21])
        prep = [p21, p22, p23, a_mag, p3, k0, k1, ld["x1"], ld["x2"], ld["iota"], ld["d16"]]

        vi = 0
        yi = 0
        vfree = [[] for _ in range(2)]
        yfree = [[] for _ in range(2)]
        tabs_free = []
        bt_free = []
        xs_free = []
        pq_free = []
        magf_free = []
        for g in range(NG):
            c, j = g // 8, g % 8
            if j == 0:
                ku = ub_s.next()
                ub = ub_s.tiles[ku]
                d_u = io["u_chunk"](S_, c, ub, ubsem[ku], ub_s.free_tok[ku])
                kv = wv_s.next()
                wvt = wv_s.tiles[kv]
                d_wv = S_.dma("pool", lambda e, wvt=wvt, c=c: e.dma_start(out=wvt[:, :, :], in_=wv_d[:, c * 8:(c + 1) * 8, :]),
                              wvsem[kv], waits=wv_s.free_tok[kv])
                kw = ws_s.next()
                wst = ws_s.tiles[kw]
                d_ws = S_.dma("pool", lambda e, wst=wst, c=c: e.dma_start(out=wst[:, :, :], in_=wsw_d[:, c * 8:(c + 1) * 8, :]),
                              wssem[kw], waits=ws_s.free_tok[kw])
            t_k = S_.op("act", lambda e, g=g: e.activation(out=mt[:, :], in_=iota[:, :], func=AF.Identity,
                                                           scale=sm["th"][:, g:g + 1], bias=magt[:, 0:1]),
                        waits=prep + tabs_free)
            t_k2 = S_.op("dve", lambda e: e.tensor_scalar(out=mt[:, :], in0=mt[:, :], scalar1=-MAGIC, scalar2=None, op0=ALU.add),
                         waits=[t_k])
            t_m = S_.op("dve", lambda e, g=g: e.scalar_tensor_tensor(out=mt[:, :], in0=iota[:, :], scalar=sm["th"][:, g:g + 1],
                                                                     in1=mt[:, :], op0=ALU.mult, op1=ALU.subtract),
                        waits=[t_k2])
            t_s2 = S_.op("act", lambda e: e.activation(out=s2[:, :], in_=mt[:, :], func=AF.Sin, scale=nsgnpi[:, 0:1]),
                         waits=[t_m] + tabs_free)
            t_m2 = S_.op("act", lambda e: e.activation(out=mt[:, :], in_=mt[:, :], func=AF.Abs), waits=[t_s2])
            t_c2 = S_.op("act", lambda e: e.activation(out=c2[:, :], in_=mt[:, :], func=AF.Sin, scale=TWO_PI, bias=negpi[:, 0:1]),
                         waits=[t_m2] + tabs_free)
            t_mg = S_.op("act", lambda e, g=g: e.activation(out=magf[:, :], in_=iota[:, :], func=AF.Identity, scale=0.0,
                                                            bias=sm["mag"][:, g:g + 1]), waits=prep + magf_free)
            kt = wtmp.next()
            wt_ = wtmp.tiles[kt]
            k1g = w1g.next()
            W1 = w1g.tiles[k1g]
            W2 = w2g.tiles[k1g]
            q1 = S_.op("dve", lambda e, wt_=wt_, g=g: e.tensor_scalar(out=wt_[:, :], in0=x2[:, g, :], scalar1=sm["nzi"][:, g:g + 1],
                                                                      scalar2=None, op0=ALU.mult), waits=prep + wtmp.free_tok[kt])
            q2 = S_.op("dve", lambda e, wt_=wt_, W1=W1, g=g: e.scalar_tensor_tensor(
                out=W1[:, :], in0=x1[:, g, :], scalar=sm["zrs"][:, g:g + 1], in1=wt_[:, :], op0=ALU.mult, op1=ALU.add),
                waits=[q1] + w1g.free_tok[k1g])
            q3 = S_.op("dve", lambda e, wt_=wt_, g=g: e.tensor_scalar(out=wt_[:, :], in0=x2[:, g, :], scalar1=sm["nzrs"][:, g:g + 1],
                                                                      scalar2=None, op0=ALU.mult), waits=[q2])
            q4 = S_.op("dve", lambda e, wt_=wt_, W2=W2, g=g: e.scalar_tensor_tensor(
                out=W2[:, :], in0=x1[:, g, :], scalar=sm["nzi"][:, g:g + 1], in1=wt_[:, :], op0=ALU.mult, op1=ALU.add),
                waits=[q3])
            wtmp.free_tok[kt] = [q4]
            bts = []
            last_v = None
            for t8 in range(NT):
                sl = slice(t8 * 512, (t8 + 1) * 512)
                vb = vi % 2
                vi += 1
                t_v = S_.op("pe", lambda e, vb=vb, wvt=wvt, ub=ub, j=j, sl=sl: e.matmul(
                    vbank[vb][:, :], lhsT=wvt[:, j, :], rhs=ub[:, sl], start=True, stop=True),
                    waits=[d_u, d_wv] + vfree[vb])
                t_w = S_.op("pe", lambda e, vb=vb, wst=wst, ub=ub, j=j, sl=sl: e.matmul(
                    wbank[vb][:, :], lhsT=wst[:, j, :], rhs=ub[:, sl], start=True, stop=True), waits=[d_ws])
                last_v = t_w
                ka = t1_s.next()
                ta = t1_s.tiles[ka]
                tb = t2_s.tiles[ka]
                t_1 = S_.op("dve", lambda e, ta=ta, vb=vb, sl=sl: e.tensor_tensor(out=ta[:, :], in0=vbank[vb][:, :], in1=c2[:, sl],
                                                                               op=ALU.mult), waits=[t_v, t_c2] + t1_s.free_tok[ka])
                t_2 = S_.op("dve", lambda e, tb=tb, vb=vb, sl=sl: e.tensor_tensor(out=tb[:, :], in0=wbank[vb][:, :], in1=s2[:, sl],
                                                                               op=ALU.mult), waits=[t_w, t_s2])
                vfree[vb] = [t_1, t_2]
                t_3 = S_.op("pool", lambda e, ta=ta, tb=tb, sl=sl: e.tensor_tensor(out=btl[:, sl], in0=ta[:, :], in1=tb[:, :], op=ALU.add),
                            waits=[t_1, t_2] + bt_free)
                t1_s.free_tok[ka] = [t_3]
                bts.append(t_3)
            if j == 7:
                ub_s.free_tok[ku] = [last_v]
                wv_s.free_tok[kv] = [last_v]
                ws_s.free_tok[kw] = [last_v]
            t_sc = S_.op("dve", lambda e: e.tensor_tensor_scan(out=xs[:, :], data0=magf[:, :], data1=btl[:, :], initial=0.0,
                                                              op0=ALU.mult, op1=ALU.add), waits=bts + [t_mg] + xs_free)
            bt_free = [t_sc]
            magf_free = [t_sc]
            t_p = S_.op("dve", lambda e: e.tensor_tensor(out=pp[:, :], in0=xs[:, :], in1=c2[:, :], op=ALU.mult),
                        waits=[t_sc] + pq_free)
            t_q = S_.op("dve", lambda e: e.tensor_tensor(out=qq[:, :], in0=xs[:, :], in1=s2[:, :], op=ALU.mult),
                        waits=[t_sc] + pq_free)
            xs_free = [t_p, t_q]
            tabs_free = [t_p, t_q]
            last_y = None
            grp_toks = []
            for t8 in range(NT):
                sl = slice(t8 * 512, (t8 + 1) * 512)
                yb = yi % 2
                yi += 1
                t_y1 = S_.op("pe", lambda e, yb=yb, W1=W1, sl=sl: e.matmul(ybank[yb][0:16, :], lhsT=W1[:, :], rhs=pp[:, sl],
                                                                         start=True, stop=False), waits=[t_p, q2] + yfree[yb])
                t_y2 = S_.op("pe", lambda e, yb=yb, W2=W2, sl=sl: e.matmul(ybank[yb][0:16, :], lhsT=W2[:, :], rhs=qq[:, sl],
                                                                         start=False, stop=True), waits=[t_q, q4])
                last_y = t_y2
                kk_ = u32_s.next()
                ut = u32_s.tiles[kk_]
                d_uu = S_.dma("sp", lambda e, ut=ut, g=g, t8=t8: e.dma_start(out=ut[:, :], in_=io["u_rows_src"](e, g, t8)),
                              u32sem[kk_], waits=u32_s.free_tok[kk_])
                ko = yo_s.next()
                yt = yo_s.tiles[ko]
                t_ev = S_.op("dve", lambda e, yt=yt, ut=ut, yb=yb, g=g: e.scalar_tensor_tensor(
                    out=yt[:, :], in0=ut[:, :], scalar=d16[:, g:g + 1], in1=ybank[yb][0:16, :], op0=ALU.mult, op1=ALU.add),
                    waits=[t_y2, d_uu] + yo_s.free_tok[ko])
                yfree[yb] = [t_ev]
                u32_s.free_tok[kk_] = [t_ev]
                d_o = S_.dma("sp", lambda e, yt=yt, g=g, t8=t8: e.dma_start(out=io["y_dst"](g, t8), in_=yt[:, :]),
                             yosem[ko], waits=[t_ev])
                yo_s.free_tok[ko] = [d_o]
                grp_toks.append(d_o)
            pq_free = [last_y]
            w1g.free_tok[k1g] = [last_y]
            io["after_group"](g, grp_toks)
            if "tick" in io:
                io["tick"]()
        S_.wait_only("sp", [(sm_, S_.dsem_cnt[id(sm_)]) for sm_ in yosem])


def s5_host_layout(a_re, a_im, log_dt, b_re, b_im, c_re, c_im, d, g0, NG):
    gs = slice(g0, g0 + NG)
    wv = np.zeros((128, NG, 128), np.float32)
    wsw = np.zeros((128, NG, 128), np.float32)
    for gl in range(NG):
        j = gl % 8
        br = b_re[g0 + gl].T
        bi = b_im[g0 + gl].T
        wv[16 * j:16 * j + 16, gl, 0:64] = br
        wv[16 * j:16 * j + 16, gl, 64:128] = bi
        wsw[16 * j:16 * j + 16, gl, 0:64] = bi
        wsw[16 * j:16 * j + 16, gl, 64:128] = br
    cr = c_re[gs].transpose(2, 0, 1)
    ci = c_im[gs].transpose(2, 0, 1)
    x1 = np.ascontiguousarray(np.concatenate([cr, ci], 0))
    x2 = np.ascontiguousarray(np.concatenate([ci, cr], 0))
    lr = np.ascontiguousarray(np.concatenate([a_re[gs].T, a_re[gs].T], 0))
    li = np.ascontiguousarray(np.concatenate([a_im[gs].T, a_im[gs].T], 0))
    ldt = np.ascontiguousarray(np.broadcast_to(log_dt[gs][None, :], (128, NG)))
    d16 = np.ascontiguousarray(d[g0 * 16:(g0 + NG) * 16].reshape(NG, 16).T)
    return {"wv": wv, "wsw": wsw, "x1": x1, "x2": x2, "lr": lr, "li": li, "ldt": ldt, "d16": d16}


def s5_consts(SEQ):
    sgn = np.concatenate([np.ones((64, 1), np.float32), -np.ones((64, 1), np.float32)], 0)
    iota = np.ascontiguousarray(np.broadcast_to(np.arange(SEQ, dtype=np.float32)[None, :], (128, SEQ)))
    return {"sgn": sgn, "iota": iota}


def _maxtoks(toks):
    best = {}
    for t in toks:
        if t is None:
            continue
        k = id(t[0])
        if k not in best or best[k][1] < t[1]:
            best[k] = t
    return list(best.values())


RG_PAIRS = [[0, 1], [2, 3], [4, 5], [6, 7]]


def build_fused():
    nc = bass.Bass("TRN2", target_bir_lowering=False)
    T = S // 2
    din = lambda n, shp, d=F32: nc.dram_tensor("i_" + n, list(shp), d, kind="ExternalInput").ap()
    dint = lambda n, shp, d=F32: nc.dram_tensor("t_" + n, list(shp), d, kind="Internal")
    xT = din("xT", [D, T])
    w_in0 = din("w_in0", [D, 8192])
    s5p = {"wv": din("wv", [128, 64, 128]), "wsw": din("wsw", [128, 64, 128]), "x1": din("x1", [128, 64, 16]),
           "x2": din("x2", [128, 64, 16]), "lr": din("lr", [128, 64]), "li": din("li", [128, 64]),
           "ldt": din("ldt", [128, 64]), "d16": din("d16", [16, 64]), "sgn": din("sgn", [128, 1]),
           "iota": din("iota", [128, S])}
    tri_s = din("tri_s", [128, 128], BF16)
    umat = din("umat", [128, 128], BF16)
    tri_i = din("tri_i", [128, 128], BF16)
    ident = din("ident", [128, 128])
    bfv = din("bf", [16, 1])
    c0 = {"hres": xT, "w_glu": din("w_glu", [2048, 2048]), "w_out": din("w_out0", [D, D]), "lnp": din("lnp0", [128, 4, 32]),
          "w1": din("w1_0", [D, 4 * D]), "w2": din("w2_0", [4 * D, D]), "w_fox": din("w_fox", [D, FOX_IN])}
    c1 = {"w_out": din("w_out1", [D, D]), "lnp": din("lnp1", [128, 4, 32]),
          "w1": din("w1_1", [D, 4 * D]), "w2": din("w2_1", [4 * D, D])}
    outT = nc.dram_tensor("outT", [D, T], F32, kind="ExternalOutput").ap()

    src_u = dint("src_u", [4, 2, 1024, 512]); G_u = dint("G_u", [4, 2, 2, 1024, 512])
    src_q = dint("src_q", [4, 2048, 512], BF16); G_q = dint("G_q", [4, 2, 2048, 512], BF16)
    src_k = dint("src_k", [4, 2048, 512], BF16); G_k = dint("G_k", [4, 2, 2048, 512], BF16)
    src_v = dint("src_v", [4, 512, 2048], BF16); G_v = dint("G_v", [4, 2, 512, 2048], BF16)
    src_ys = dint("src_ys", [8, 128, S]); G_ys = dint("G_ys", [8, 2, 128, S])
    src_yb = dint("src_yb", [4, 256, S], BF16); G_yb = dint("G_yb", [4, 2, 256, S], BF16)
    h1T = dint("h1T", [D, T]).ap()
    zbuf = dint("zbuf", [D, TT]).ap()
    hmid = dint("hmid", [D, TT]).ap()
    src_q1 = dint("src_q1", [4, 2, 2048, 512], BF16); G_q1 = dint("G_q1", [4, 2, 2, 2048, 512], BF16)
    src_k1 = dint("src_k1", [4, 2, 2048, 512], BF16); G_k1 = dint("G_k1", [4, 2, 2, 2048, 512], BF16)
    src_v1 = dint("src_v1", [4, 2, 256, D], BF16); G_v1 = dint("G_v1", [4, 2, 2, 256, D], BF16)
    src_f = dint("src_f", [32, T]); G_f = dint("G_f", [2, 32, T])
    src_y1 = dint("src_y1", [8, 256, S], BF16); G_y1 = dint("G_y1", [8, 2, 256, S], BF16)
    c0.update({"hout": h1T, "zbuf": zbuf, "hmid": hmid})
    c1.update({"hres": h1T, "hout": outT, "zbuf": zbuf, "hmid": hmid})

    with contextlib.ExitStack() as gst:
        S_ = Sched(nc, gst)
        ccsem = S_.new_sem("ccsem")

        def coll(src2d, dst2d, toks):
            return S_.cc(lambda e: e.collective_compute("AllGather", ALU.bypass, replica_groups=RG_PAIRS,
                                                        ins=[src2d.opt()], outs=[dst2d.opt()]), ccsem, _maxtoks(toks))

        def g2(G, *idx):
            a = G.ap()
            for i_ in idx:
                a = a[i_]
            return a.rearrange("r p t -> (r p) t")

        def s2_(Sx, *idx):
            a = Sx.ap()
            for i_ in idx:
                a = a[i_]
            return a

        par_cache = {}

        def par_of(e):
            k = id(e)
            if k not in par_cache:
                par_cache[k] = (e, e.partition_id() % 2)
            return par_cache[k][1]

        def phase_begin():
            tot = (ccsem, S_.dsem_cnt[id(ccsem)])
            if tot[1] > 0:
                for en in ENGS:
                    S_.wait_only(en, [tot])

        def a0_dst(col0, tt):
            if col0 < 2048:
                ci, w0 = col0 // 1024, col0 % 1024
                return src_u.ap()[tt, ci, w0:w0 + 512, :].rearrange("(j p) t -> p j t", p=128)
            if col0 < 4096:
                return src_q.ap()[tt, col0 - 2048:col0 - 2048 + 512, :].rearrange("(j p) t -> p j t", p=128)
            if col0 < 6144:
                return src_k.ap()[tt, col0 - 4096:col0 - 4096 + 512, :].rearrange("(j p) t -> p j t", p=128)
            return src_v.ap()[tt, :, col0 - 6144:col0 - 6144 + 512].rearrange("(j p) c -> p j c", p=128)

        def a0_after(tt, toks):
            for ci in range(2):
                coll(s2_(src_u, tt, ci), g2(G_u, tt, ci), toks)
            coll(s2_(src_q, tt), g2(G_q, tt), toks)
            coll(s2_(src_k, tt), g2(G_k, tt), toks)
            coll(s2_(src_v, tt), g2(G_v, tt), toks)

        with contextlib.ExitStack() as st:
            emit_a0(S_, nc, st, xT, w_in0, T, {"a0_dst": a0_dst, "a0_after_tile": a0_after})
            S_.emit()

        msel = S_.new_sem("msel")
        wcsem = S_.new_sem("wcsem")
        jobs = []
        wb = {}

        def add_cast(name, src_ap, kr):
            K_, N_ = src_ap.shape
            t = dint("wb_" + name, [K_, N_], BF16)
            wb[name] = t.ap()
            for k0 in range(0, K_, kr):
                jobs.append((t.ap()[k0:k0 + kr, :], src_ap[k0:k0 + kr, :]))

        add_cast("w_glu", c0["w_glu"], 1024)
        add_cast("w_out0", c0["w_out"], 512)
        add_cast("w1_0", c0["w1"], 128)
        add_cast("w2_0", c0["w2"], 512)
        add_cast("w_fox", c0["w_fox"], 128)
        n_jobs_l0 = len(jobs)
        add_cast("w_out1", c1["w_out"], 512)
        add_cast("w1_1", c1["w1"], 128)
        add_cast("w2_1", c1["w2"], 512)
        c0.update({"w_glu": wb["w_glu"], "w_out": wb["w_out0"], "w1": wb["w1_0"], "w2": wb["w2_0"], "w_fox": wb["w_fox"]})
        c1.update({"w_out": wb["w_out1"], "w1": wb["w1_1"], "w2": wb["w2_1"]})
        job_state = {"i": 0, "acc": 0.0, "rate": 0.0}

        def issue_job():
            i = job_state["i"]
            if i >= len(jobs):
                return
            dst, src = jobs[i]
            waits = [(wcsem, 16 * (i - 2))] if i >= 3 else []
            S_.dma("pool", lambda e: e.dma_start(out=dst, in_=src, max_dma_last_dim=4096), wcsem, waits=waits)
            job_state["i"] = i + 1

        def tick():
            job_state["acc"] += job_state["rate"]
            while job_state["acc"] >= 1.0:
                job_state["acc"] -= 1.0
                issue_job()

        def dyn_copy(eng, dst_ap, src_fn):
            return S_.dma(eng, lambda e: e.dma_start(out=dst_ap, in_=src_fn(par_of(e))), msel)

        M_u = dint("M_u", [4, 2, 1024, 512])
        M_q = dint("M_q", [4, 2, 1024, 512], BF16)
        M_k = dint("M_k", [4, 2, 1024, 512], BF16)
        M_v = dint("M_v", [2, 4, 512, 1024], BF16)
        phase_begin()
        sel_a = [
            dyn_copy("sp", M_u.ap().rearrange("tt r p t -> tt (r p t)"),
                     lambda par: G_u.ap()[:, bass.ds(par, 1), :, :, :].rearrange("tt 1 r p t -> tt (r p t)")),
            dyn_copy("sp", M_q.ap().rearrange("tt r p t -> (tt r) (p t)"),
                     lambda par: G_q.ap().rearrange("tt r (c p) t -> (tt r) c (p t)", c=2)[:, bass.ds(par, 1), :].rearrange("a 1 b -> a b")),
            dyn_copy("sp", M_k.ap().rearrange("tt r p t -> (tt r) (p t)"),
                     lambda par: G_k.ap().rearrange("tt r (c p) t -> (tt r) c (p t)", c=2)[:, bass.ds(par, 1), :].rearrange("a 1 b -> a b")),
        ]
        for r in range(2):
            sel_a.append(dyn_copy("sp", M_v.ap()[r],
                                  lambda par, r=r: G_v.ap().rearrange("tt r k (c d) -> r tt k c d", c=2)[r, :, :, bass.ds(par, 1), :]
                                  .rearrange("tt k 1 d -> tt k d")))
        for en in ENGS:
            S_.wait_only(en, sel_a)

        def u_chunk(S__, c, ub, sem, waits):
            t = None
            for r in range(2):
                t = S__.dma("pool", lambda e, r=r: e.dma_start(
                    out=ub[:, r * 2048:(r + 1) * 2048].rearrange("p (tt t) -> p tt t", tt=4),
                    in_=M_u.ap()[:, r, c * 128:(c + 1) * 128, :].rearrange("tt p t -> p tt t")), sem, waits=waits)
            return t

        def u_rows_src(e, g, t8):
            return M_u.ap()[t8 % 4, t8 // 4, g * 16:(g + 1) * 16, :]

        s5_acc = []

        def s5_after(g, toks):
            s5_acc.extend(toks)
            if g % 8 == 7:
                coll(s2_(src_ys, g // 8), g2(G_ys, g // 8), s5_acc)
                del s5_acc[:]

        n_s5_jobs = int(len(jobs) * 0.53)
        job_state["rate"] = n_s5_jobs / 64.0
        with contextlib.ExitStack() as st:
            emit_s5(S_, nc, st, s5p, {
                "tick": tick,
                "u_chunk": u_chunk, "u_rows_src": u_rows_src,
                "y_dst": lambda g, t8: src_ys.ap()[g // 8, (g % 8) * 16:(g % 8) * 16 + 16, t8 * 512:(t8 + 1) * 512],
                "after_group": s5_after}, 64, S)
            S_.emit()

        def mk_loaders(Mq, Mk, Mv):
            def load_k(S__, h, ktile, sem, waits):
                t = None
                for r in range(2):
                    t = S__.dma("sp", lambda e, r=r: e.dma_start(
                        out=ktile[:, r * 2048:(r + 1) * 2048].rearrange("p (tt t) -> p tt t", tt=4),
                        in_=Mk.ap()[:, r, h * 128:(h + 1) * 128, :].rearrange("tt p t -> p tt t")), sem, waits=waits)
                return t

            def load_q(S__, h, I, qtile, sem, waits):
                return S__.dma("sp", lambda e: e.dma_start(out=qtile[:, :], in_=Mq.ap()[I % 4, I // 4, h * 128:(h + 1) * 128, :]),
                               sem, waits=waits)

            def load_v(S__, h, vtile, sem, waits):
                vv_ = Mv.ap().rearrange("r a k d -> (r a k) d")
                return S__.dma("sp", lambda e: e.dma_start(
                    out=vtile[:, :, :], in_=vv_[:, h * 128:(h + 1) * 128].rearrange("(j p) d -> p j d", p=128)), sem, waits=waits)
            return load_k, load_q, load_v

        sb_acc = []

        def sb_after(h, toks):
            sb_acc.extend(toks)
            if h % 2 == 1:
                coll(s2_(src_yb, h // 2), g2(G_yb, h // 2), sb_acc)
                del sb_acc[:]

        lk, lq, lv = mk_loaders(M_q, M_k, M_v)
        job_state["rate"] = (len(jobs) - job_state["i"]) / (8 * 144.0) * 1.05
        with contextlib.ExitStack() as st:
            fs = []
            emit_sb(S_, nc, st, tri_s, umat, {
                "tick": tick,
                "load_k": lk, "load_q": lq, "load_v": lv,
                "y_dst": lambda h, I: src_yb.ap()[h // 2, (h % 2) * 128:(h % 2) * 128 + 128, I * 512:(I + 1) * 512],
                "after_head": sb_after}, 8, S, fs)
            while job_state["i"] < len(jobs):
                issue_job()
            S_.wait_only("sp", [(sm, S_.dsem_cnt[id(sm)]) for sm in fs])
            S_.emit()

        M_ys = dint("M_ys", [8, 2, 128, T])
        M_yb = dint("M_yb", [4, 2, 256, T], BF16)
        phase_begin()
        sel_b = [
            dyn_copy("sp", M_ys.ap().rearrange("ci r p t -> (ci r p) t"),
                     lambda par: G_ys.ap().rearrange("ci r p (h t) -> (ci r p) h t", h=2)[:, bass.ds(par, 1), :].rearrange("a 1 t -> a t")),
            dyn_copy("sp", M_yb.ap().rearrange("ci r p t -> (ci r p) t"),
                     lambda par: G_yb.ap().rearrange("ci r p (h t) -> (ci r p) h t", h=2)[:, bass.ds(par, 1), :].rearrange("a 1 t -> a t")),
        ]
        for en in ENGS:
            S_.wait_only(en, sel_b + [(wcsem, 16 * n_jobs_l0)])

        def yssm_src(e, c, t0):
            return M_ys.ap()[c % 8, c // 8, :, t0:t0 + 512]

        def ysb_src(e, cc, t0):
            w = (cc % 2) * 128
            return M_yb.ap()[(cc % 8) // 2, cc // 8, w:w + 128, t0:t0 + 512]

        def fox_dst(col0, j, tt):
            if col0 < 2 * D:
                Sx = src_q1 if col0 < D else src_k1
                row = (col0 % D) + j * 128
                return Sx.ap()[tt, row // 2048, row % 2048:row % 2048 + 128, :]
            cc0 = col0 - 2 * D
            return src_v1.ap()[tt, j // 2, (j % 2) * 128:(j % 2) * 128 + 128, cc0:cc0 + 512]

        def c0_after(tt, toks):
            for ci in range(2):
                coll(s2_(src_q1, tt, ci), g2(G_q1, tt, ci), toks)
                coll(s2_(src_k1, tt, ci), g2(G_k1, tt, ci), toks)
                coll(s2_(src_v1, tt, ci), g2(G_v1, tt, ci), toks)
            if tt == 3:
                coll(src_f.ap(), G_f.ap().rearrange("r p t -> (r p) t"), toks)

        with contextlib.ExitStack() as st:
            emit_c(S_, nc, st, 0, T, c0, {"yssm_src": yssm_src, "ysb_src": ysb_src, "fox_dst": fox_dst,
                                          "f_dst": lambda tt: src_f.ap()[:, tt * 512:(tt + 1) * 512],
                                          "c_after_tile": c0_after})
            S_.emit()

        M_q1 = dint("M_q1", [4, 2, 2048, 512], BF16)
        M_k1 = dint("M_k1", [4, 2, 2048, 512], BF16)
        M_v1 = dint("M_v1", [2, 8, 256, 2048], BF16)
        M_f = dint("M_f", [2, 16, T])
        phase_begin()
        sel_c = [
            dyn_copy("act", M_q1.ap().rearrange("tt r p t -> tt (r p t)"),
                     lambda par: G_q1.ap()[:, bass.ds(par, 1), :, :, :].rearrange("tt 1 r p t -> tt (r p t)")),
            dyn_copy("act", M_k1.ap().rearrange("tt r p t -> tt (r p t)"),
                     lambda par: G_k1.ap()[:, bass.ds(par, 1), :, :, :].rearrange("tt 1 r p t -> tt (r p t)")),
            dyn_copy("act", M_f.ap().rearrange("r h t -> r (h t)"),
                     lambda par: G_f.ap().rearrange("r (c h) t -> r c (h t)", c=2)[:, bass.ds(par, 1), :].rearrange("r 1 a -> r a")),
        ]
        for r in range(2):
            sel_c.append(dyn_copy("act", M_v1.ap()[r],
                                  lambda par, r=r: G_v1.ap().rearrange("tt ci r k (c d) -> r (tt ci) k c d", c=2)[r, :, :, bass.ds(par, 1), :]
                                  .rearrange("a k 1 d -> a k d")))
        for en in ENGS:
            S_.wait_only(en, sel_c)

        def fx_load_f(S__, fl, sem):
            return S__.dma("sp", lambda e: e.dma_start(out=fl[:, :].rearrange("h (r t) -> h r t", r=2),
                                                       in_=M_f.ap().rearrange("r h t -> h r t")), sem)

        fx_acc = []

        def fx_after(h, toks):
            fx_acc.extend(toks)
            if h % 2 == 1:
                coll(s2_(src_y1, h // 2), g2(G_y1, h // 2), fx_acc)
                del fx_acc[:]

        lk1, lq1, lv1 = mk_loaders(M_q1, M_k1, M_v1)
        with contextlib.ExitStack() as st:
            emit_fox(S_, nc, st, {"bf": bfv, "ident": ident, "tri": tri_i}, {
                "load_f": fx_load_f, "load_k": lk1, "load_q": lq1, "load_v": lv1,
                "y_dst": lambda h, I: src_y1.ap()[h // 2, (h % 2) * 128:(h % 2) * 128 + 128, I * 512:(I + 1) * 512],
                "after_head": fx_after}, 16, S)
            S_.emit()

        M_y1 = dint("M_y1", [8, 2, 256, T], BF16)
        phase_begin()
        sel_d = [dyn_copy("sp", M_y1.ap().rearrange("ci r p t -> (ci r p) t"),
                          lambda par: G_y1.ap().rearrange("ci r p (h t) -> (ci r p) h t", h=2)[:, bass.ds(par, 1), :].rearrange("a 1 t -> a t"))]
        for en in ENGS:
            S_.wait_only(en, sel_d + [(wcsem, 16 * len(jobs))])

        def yin_src(e, cc, t0):
            w = (cc % 2) * 128
            return M_y1.ap()[(cc % 16) // 2, cc // 16, w:w + 128, t0:t0 + 512]

        with contextlib.ExitStack() as st:
            emit_c(S_, nc, st, 1, T, c1, {"yin_src": yin_src})
            S_.emit()
    return nc


def _lnp(mg, mb, fg, fb):
    a = np.stack([mg, mb, fg, fb]).astype(np.float32)
    return np.ascontiguousarray(a.reshape(4, 32, 128).transpose(2, 0, 1))


def kernel(x, even_w_in, ssm_a_re, ssm_a_im, ssm_log_dt, ssm_b_re, ssm_b_im, ssm_c_re, ssm_c_im, ssm_d,
           ssm_w_glu, even_w_out, fox_w_in, fox_b_f, fox_w_out, ln_mix_g, ln_mix_b, mlp_w1, mlp_w2,
           ln_ffn_g, ln_ffn_b):
    import ml_dtypes
    bf16 = ml_dtypes.bfloat16
    f32 = lambda a: np.ascontiguousarray(np.asarray(a, dtype=np.float32))
    x = np.asarray(x, dtype=np.float32)
    T = S // 2
    consts = s5_consts(S)
    lay = [s5_host_layout(f32(ssm_a_re[0]), f32(ssm_a_im[0]), f32(ssm_log_dt[0]), f32(ssm_b_re[0]), f32(ssm_b_im[0]),
                          f32(ssm_c_re[0]), f32(ssm_c_im[0]), f32(ssm_d[0]), hh * 64, 64) for hh in range(2)]
    shared = {
        "i_w_in0": f32(even_w_in[0]), "i_sgn": consts["sgn"], "i_iota": consts["iota"],
        "i_tri_s": np.triu(np.ones((128, 128), np.float32), 1).astype(bf16),
        "i_umat": np.tril(np.ones((128, 128), np.float32), -1).astype(bf16),
        "i_tri_i": np.triu(np.ones((128, 128), np.float32)).astype(bf16),
        "i_ident": np.eye(128, dtype=np.float32),
        "i_w_glu": f32(ssm_w_glu[0]), "i_w_out0": f32(even_w_out[0]),
        "i_lnp0": _lnp(ln_mix_g[0], ln_mix_b[0], ln_ffn_g[0], ln_ffn_b[0]),
        "i_w1_0": f32(mlp_w1[0]), "i_w2_0": f32(mlp_w2[0]), "i_w_fox": f32(fox_w_in[0]),
        "i_w_out1": f32(fox_w_out[0]), "i_lnp1": _lnp(ln_mix_g[1], ln_mix_b[1], ln_ffn_g[1], ln_ffn_b[1]),
        "i_w1_1": f32(mlp_w1[1]), "i_w2_1": f32(mlp_w2[1]),
    }
    bfull = f32(fox_b_f[0])
    maps = []
    for c in range(NCORES):
        b, h = c // 2, c % 2
        m = dict(shared)
        m["i_xT"] = np.ascontiguousarray(x[b, h * T:(h + 1) * T, :].T)
        for k_, v_ in lay[h].items():
            m["i_" + k_] = v_
        m["i_bf"] = np.ascontiguousarray(bfull[h * 16:(h + 1) * 16].reshape(16, 1))
        maps.append(m)
    res = run_bass_kernel_spmd(build_fused(), maps, core_ids=list(range(NCORES))).results
    out = np.empty((B, S, D), np.float32)
    for c in range(NCORES):
        b, h = c // 2, c % 2
        out[b, h * T:(h + 1) * T, :] = res[c]["outT"].T
    return out
```
